# Optimizing a Trainium2 kernel written in Bass

```python
import math
import jax, jax.numpy as jnp
from jax import lax
import numpy as np

D_MODEL = 4096
BATCH = 4
SEQ = 2048
DEPTH = 2
DEC_BATCH = 1
DEC_SEQ = 16384
PAST_LEN = 128

HEAD_DIM = 128
GRID_W = 64
A_HEADS = 12
A_KV_HEADS = 4
A_GROUP = A_HEADS // A_KV_HEADS
B_HEADS = 8
C_HEADS = 6
A_Q = A_HEADS * HEAD_DIM
A_KV = A_KV_HEADS * HEAD_DIM
B_W = B_HEADS * HEAD_DIM
C_QK = C_HEADS * 2 * HEAD_DIM
C_V = C_HEADS * 2 * HEAD_DIM
D_MIX = A_Q + B_W + C_V
D_IN = A_Q + 2 * A_KV + 3 * B_W + 2 * C_QK + C_V + D_MIX
SPLITS = list(np.cumsum([A_Q, A_KV, A_KV, B_W, B_W, B_W, C_QK, C_QK, C_V])[:].tolist())
Q_BLOCK = 128
NA_ROWS = 8
NA_COLS = 16
AXIAL_THETA = 10000.0
ROPE_THETA = 500000.0
ROPE_DIMS = HEAD_DIM // 4
EPS = 1e-6
NEG_INF = -1e30

kernel_name = "hybrid_parallel_head_group_encoder"


def _rmsnorm(x, g):
    xf = x.astype(jnp.float32)
    xf = xf * lax.rsqrt(jnp.mean(xf * xf, axis=-1, keepdims=True) + EPS)
    return xf.astype(x.dtype) * g


def _angles(pos, dim, theta):
    inv = jnp.power(theta, -jnp.arange(0, dim, 2, dtype=jnp.float32) / dim)
    return pos.astype(jnp.float32)[:, None] * inv[None, :]


def _rope(x, ang):
    n = ang.shape[-1]
    xf = x.astype(jnp.float32)
    x1, x2 = xf[..., :n], xf[..., n:]
    c = jnp.cos(ang)[:, None, :]
    s = jnp.sin(ang)[:, None, :]
    return jnp.concatenate([x1 * c - x2 * s, x2 * c + x1 * s], axis=-1).astype(x.dtype)


def _axial_rope(x, ang_row, ang_col):
    half = HEAD_DIM // 2
    return jnp.concatenate([_rope(x[..., :half], ang_row), _rope(x[..., half:], ang_col)], axis=-1)


def _partial_rope(x, ang):
    return jnp.concatenate([_rope(x[..., :ROPE_DIMS], ang), x[..., ROPE_DIMS:]], axis=-1)


def _gqa_attention(q, k, v):
    B, L, H, D = q.shape
    nb = L // Q_BLOCK
    scale = 1.0 / math.sqrt(D)
    qb = q.reshape(B, nb, Q_BLOCK, A_KV_HEADS, A_GROUP, D).swapaxes(0, 1)

    def block(qi):
        s = jnp.einsum('bqkgd,bskd->bkgqs', qi, k).astype(jnp.float32) * scale
        p = jax.nn.softmax(s, axis=-1).astype(v.dtype)
        return jnp.einsum('bkgqs,bskd->bqkgd', p, v)

    o = lax.map(block, qb)
    return o.swapaxes(0, 1).reshape(B, L, H * D)


def _neighborhood_attention(q, k, v, rel_bias):
    B, L, H, D = q.shape
    rows = L // GRID_W
    wr = min(NA_ROWS, rows)
    scale = 1.0 / math.sqrt(D)
    r = jnp.arange(rows)
    row_start = jnp.clip(r - wr // 2, 0, rows - wr)
    band = row_start[:, None] + jnp.arange(wr)[None, :]
    c = jnp.arange(GRID_W)
    col_start = jnp.clip(c - NA_COLS // 2, 0, GRID_W - NA_COLS)
    col_ok = (c[None, :] >= col_start[:, None]) & (c[None, :] < col_start[:, None] + NA_COLS)
    qg = q.reshape(B, rows, GRID_W, H, D)
    kg = k.reshape(B, rows, GRID_W, H, D)[:, band]
    vg = v.reshape(B, rows, GRID_W, H, D)[:, band]
    s = jnp.einsum('brqhd,brikhd->brhqik', qg, kg).astype(jnp.float32) * scale
    dr = (band - r[:, None]) + (NA_ROWS - 1)
    dc = jnp.clip(c[None, :] - c[:, None], -(NA_COLS - 1), NA_COLS - 1) + (NA_COLS - 1)
    bias = rel_bias[:, dr[:, None, :, None], dc[None, :, None, :]]
    bias = jnp.transpose(bias, (1, 0, 2, 3, 4)).astype(jnp.float32)
    s = jnp.where(col_ok[:, None, :], s + bias[None], NEG_INF)
    p = jax.nn.softmax(s, axis=(-2, -1)).astype(v.dtype)
    o = jnp.einsum('brhqik,brikhd->brqhd', p, vg)
    return o.reshape(B, L, H * D)


def _diff_attention(q1, q2, k1, k2, v, lam):
    B, L, H, D = q1.shape
    nb = L // Q_BLOCK
    scale = 1.0 / math.sqrt(D)
    q1b = q1.reshape(B, nb, Q_BLOCK, H, D).swapaxes(0, 1)
    q2b = q2.reshape(B, nb, Q_BLOCK, H, D).swapaxes(0, 1)

    def block(qs):
        q1i, q2i = qs
        s1 = jnp.einsum('bqhd,bshd->bhqs', q1i, k1).astype(jnp.float32) * scale
        s2 = jnp.einsum('bqhd,bshd->bhqs', q2i, k2).astype(jnp.float32) * scale
        p = jax.nn.softmax(s1, axis=-1) - lam * jax.nn.softmax(s2, axis=-1)
        return jnp.einsum('bhqs,bshe->bqhe', p.astype(v.dtype), v)

    o = lax.map(block, (q1b, q2b))
    return o.swapaxes(0, 1).reshape(B, L, H, 2 * D)


def _layer(x, layer_idx, pre_g, post_g, w_in, w_out, qn_g, kn_g, rel_bias,
           lq1, lk1, lq2, lk2, subln_g):
    B, L, _ = x.shape
    h = _rmsnorm(x, pre_g)
    proj = jnp.einsum('bld,de->ble', h, w_in)
    qa, ka, va, qb, kb, vb, qc, kc, vc, gate = jnp.split(proj, SPLITS, axis=-1)
    t = jnp.arange(L)

    ang_row = _angles(t // GRID_W, HEAD_DIM // 2, AXIAL_THETA)
    ang_col = _angles(t % GRID_W, HEAD_DIM // 2, AXIAL_THETA)
    qa = _axial_rope(_rmsnorm(qa.reshape(B, L, A_HEADS, HEAD_DIM), qn_g), ang_row, ang_col)
    ka = _axial_rope(_rmsnorm(ka.reshape(B, L, A_KV_HEADS, HEAD_DIM), kn_g), ang_row, ang_col)
    va = va.reshape(B, L, A_KV_HEADS, HEAD_DIM)
    out_a = _gqa_attention(qa, ka, va)

    out_b = _neighborhood_attention(qb.reshape(B, L, B_HEADS, HEAD_DIM),
                                    kb.reshape(B, L, B_HEADS, HEAD_DIM),
                                    vb.reshape(B, L, B_HEADS, HEAD_DIM), rel_bias)

    ang_t = _angles(t, ROPE_DIMS, ROPE_THETA)
    qc = qc.reshape(B, L, C_HEADS, 2, HEAD_DIM)
    kc = kc.reshape(B, L, C_HEADS, 2, HEAD_DIM)
    q1 = _partial_rope(qc[:, :, :, 0], ang_t)
    q2 = _partial_rope(qc[:, :, :, 1], ang_t)
    k1 = _partial_rope(kc[:, :, :, 0], ang_t)
    k2 = _partial_rope(kc[:, :, :, 1], ang_t)
    lam_init = 0.8 - 0.6 * math.exp(-0.3 * layer_idx)
    lam = (jnp.exp(jnp.sum(lq1.astype(jnp.float32) * lk1.astype(jnp.float32)))
           - jnp.exp(jnp.sum(lq2.astype(jnp.float32) * lk2.astype(jnp.float32))) + lam_init)
    out_c = _diff_attention(q1, q2, k1, k2, vc.reshape(B, L, C_HEADS, 2 * HEAD_DIM), lam)
    out_c = (_rmsnorm(out_c, subln_g) * (1.0 - lam_init)).reshape(B, L, C_V)

    mix = jnp.concatenate([out_a, out_b, out_c], axis=-1) * jax.nn.silu(gate)
    y = jnp.einsum('ble,ed->bld', mix, w_out)
    return x + _rmsnorm(y, post_g)


def _trunk(x, pre_norm_g, post_norm_g, w_in, w_out, a_q_norm_g, a_k_norm_g, b_rel_bias,
           c_lambda_q1, c_lambda_k1, c_lambda_q2, c_lambda_k2, c_subln_g):
    for l in range(DEPTH):
        x = _layer(x, l, pre_norm_g[l], post_norm_g[l], w_in[l], w_out[l],
                   a_q_norm_g[l], a_k_norm_g[l], b_rel_bias[l],
                   c_lambda_q1[l], c_lambda_k1[l], c_lambda_q2[l], c_lambda_k2[l], c_subln_g[l])
    return x


def setup_inputs(seed: int = 0) -> dict:
    key = jax.random.key(seed)
    ks = jax.random.split(key, 14)
    f32 = jnp.float32
    nrm = jax.random.normal
    return {
        "x_prompt": nrm(ks[0], (BATCH, SEQ, D_MODEL), f32),
        "x_sample": nrm(ks[1], (DEC_BATCH, DEC_SEQ, D_MODEL), f32),
        "pre_norm_g": 1.0 + 0.02 * nrm(ks[2], (DEPTH, D_MODEL), f32),
        "post_norm_g": 1.0 + 0.02 * nrm(ks[3], (DEPTH, D_MODEL), f32),
        "w_in": nrm(ks[4], (DEPTH, D_MODEL, D_IN), f32) * D_MODEL ** -0.5,
        "w_out": nrm(ks[5], (DEPTH, D_MIX, D_MODEL), f32) * D_MIX ** -0.5,
        "a_q_norm_g": 1.0 + 0.02 * nrm(ks[6], (DEPTH, HEAD_DIM), f32),
        "a_k_norm_g": 1.0 + 0.02 * nrm(ks[7], (DEPTH, HEAD_DIM), f32),
        "b_rel_bias": 0.1 * nrm(ks[8], (DEPTH, B_HEADS, 2 * NA_ROWS - 1, 2 * NA_COLS - 1), f32),
        "c_lambda_q1": 0.1 * nrm(ks[9], (DEPTH, HEAD_DIM), f32),
        "c_lambda_k1": 0.1 * nrm(ks[10], (DEPTH, HEAD_DIM), f32),
        "c_lambda_q2": 0.1 * nrm(ks[11], (DEPTH, HEAD_DIM), f32),
        "c_lambda_k2": 0.1 * nrm(ks[12], (DEPTH, HEAD_DIM), f32),
        "c_subln_g": 1.0 + 0.02 * nrm(ks[13], (DEPTH, 2 * HEAD_DIM), f32),
    }


def reference(x_prompt, x_sample, pre_norm_g, post_norm_g, w_in, w_out, a_q_norm_g, a_k_norm_g,
              b_rel_bias, c_lambda_q1, c_lambda_k1, c_lambda_q2, c_lambda_k2, c_subln_g):
    y_prompt = _trunk(x_prompt, pre_norm_g, post_norm_g, w_in, w_out, a_q_norm_g, a_k_norm_g,
                      b_rel_bias, c_lambda_q1, c_lambda_k1, c_lambda_q2, c_lambda_k2, c_subln_g)
    y_sample = _trunk(x_sample, pre_norm_g, post_norm_g, w_in, w_out, a_q_norm_g, a_k_norm_g,
                      b_rel_bias, c_lambda_q1, c_lambda_k1, c_lambda_q2, c_lambda_k2, c_subln_g)
    return (y_prompt, y_sample)
```

```python
import math
from contextlib import ExitStack
import numpy as np
import ml_dtypes
import concourse.bass as bass
import concourse.mybir as mybir
from concourse.bass_utils import run_bass_kernel_spmd

F32 = mybir.dt.float32
BF16 = mybir.dt.bfloat16
AF = mybir.ActivationFunctionType
ALU = mybir.AluOpType
NPBF = ml_dtypes.bfloat16

EPS = 1e-6
HD = 128
GRID_W = 64
NEG = -1e30


class Cfg:
    def __init__(self, D=4096, AH=12, AKV=4, BH=8, CH=6, NC=8, S_OWN=2048, P_OWN=1024, PB=4):
        self.D = D; self.KC = D // 128
        self.AH = AH; self.AKV = AKV; self.BH = BH; self.CH = CH
        self.NC = NC; self.S_OWN = S_OWN; self.P_OWN = P_OWN; self.PB = PB
        self.TOK = S_OWN + P_OWN
        self.LS = NC * S_OWN; self.LP = 2 * P_OWN
        widths = [AH * 128, AKV * 128, AKV * 128, BH * 128, BH * 128, BH * 128,
                  CH * 256, CH * 256, CH * 256]
        self.DMIX = AH * 128 + BH * 128 + CH * 256
        widths.append(self.DMIX)
        self.DIN = sum(widths)
        assert self.DIN % 512 == 0
        self.NG = self.DIN // 512
        names = ["AQ", "AK", "V", "P", "P", "V", "CR", "CR", "V", "G"]
        self.roles = []
        for nm, w in zip(names, widths):
            self.roles += [nm] * (w // 128)
        self.NCH = len(self.roles)
        self.fidx = {}; self.vidx = {}
        nf = nv = 0
        for c, r in enumerate(self.roles):
            if r == "V":
                self.vidx[c] = nv; nv += 1
            else:
                self.fidx[c] = nf; nf += 1
        self.NF = nf; self.NV = nv
        off = np.cumsum([0] + [w // 128 for w in widths])
        (self.c_qa, self.c_ka, self.c_va, self.c_qb, self.c_kb, self.c_vb,
         self.c_qc, self.c_kc, self.c_vc, self.c_g) = [int(o) for o in off[:10]]
        self.EM = self.DMIX // 128
        self.BLK_S = min(2048, self.LS); self.BLK_P = min(2048, self.LP)
        self.NQT_S = S_OWN // 512; self.NQT_P = P_OWN // 512
        self.NQT = self.NQT_S + self.NQT_P


FULL = Cfg()


def _os_env_bskip():
    import os
    return os.environ.get("BSKIP", "")


class Tk:
    def __init__(self, ap=None):
        self.ap = ap; self.w = None; self.r = {}; self.sem = None; self.cnt = 0


class EngW:
    def __init__(self, k, eng, name):
        self.k = k; self.e = eng; self.name = name
        self.sem = k.es.enter_context(k.nc.semaphore("p_" + name)); self.n = 0; self.seen = {}

    def wait(self, tok):
        if tok is None:
            return
        src, v = tok
        if src is self and False:
            return
        key = id(src)
        if self.seen.get(key, 0) >= v:
            return
        self.seen[key] = v
        self.e.wait_ge(src.sem, v)

    def done(self, instr):
        instr.then_inc(self.sem, 1); self.n += 1
        return (self, self.n)


class K:
    def __init__(self, nc, es):
        self.nc = nc; self.es = es; self.es0 = es
        self.pe = EngW(self, nc.tensor, "pe"); self.act = EngW(self, nc.scalar, "act")
        self.dve = EngW(self, nc.vector, "dve"); self.pool = EngW(self, nc.gpsimd, "pool")
        self.sp = EngW(self, nc.sync, "sp")
        self.nsem = 5
        self.dma_tiles = []

    def sb(self, name, shape, dt):
        return Tk(self.es.enter_context(self.nc.sbuf_tensor(name, shape, dt)))

    def ps(self, name, shape, dt):
        return Tk(self.es.enter_context(self.nc.psum_tensor(name, shape, dt)))

    def _pre(self, E, reads, writes):
        for t in reads:
            E.wait(t.w)
        for t in writes:
            E.wait(t.w)
            for x in t.r.values():
                E.wait(x)

    def _post(self, tok, reads, writes):
        for t in reads:
            t.r[id(tok[0])] = tok
        for t in writes:
            t.w = tok; t.r = {}

    def scoped(self, tk, es_scope, name, shape, dt):
        tk.ap = es_scope.enter_context(self.nc.sbuf_tensor(name, shape, dt))
        tk.w = None; tk.r = {}
        return tk

    def barrier(self):
        import os as _os
        if "b" in _os.environ.get("BSKIP", ""):
            return
        engs = (self.pe, self.act, self.dve, self.pool, self.sp)
        for E in engs:
            for F in engs:
                if F is not E and F.n:
                    E.wait((F, F.n))
            for t in self.dma_tiles:
                E.wait((t, t.cnt))

    def op(self, E, fn, reads=(), writes=()):
        self._pre(E, reads, writes)
        tok = E.done(fn())
        self._post(tok, reads, writes)
        return tok

    def pe_group(self, fns, reads=(), writes=()):
        E = self.pe
        self._pre(E, reads, writes)
        ins = None
        for f in fns:
            ins = f()
        tok = E.done(ins)
        self._post(tok, reads, writes)
        return tok

    def dma(self, Q, out_ap, in_ap, semt, reads=(), writes=(), **kw):
        self._pre(Q, reads, writes)
        if semt.sem is None:
            semt.sem = self.es0.enter_context(self.nc.semaphore("d%d" % self.nsem)); self.nsem += 1
            self.dma_tiles.append(semt)
        Q.e.dma_start(out=out_ap, in_=in_ap, **kw).then_inc(semt.sem, 16)
        semt.cnt += 16
        tok = (semt, semt.cnt)
        self._post(tok, reads, writes)
        return tok

    def finish(self):
        for t in self.dma_tiles:
            self.sp.wait((t, t.cnt))
        for E in (self.pe, self.act, self.dve, self.pool):
            if E.n:
                self.sp.wait((E, E.n))


def _angles(pos, dim, theta):
    inv = np.power(np.float32(theta), -np.arange(0, dim, 2, dtype=np.float32) / np.float32(dim)).astype(np.float32)
    return pos.astype(np.float32)[:, None] * inv[None, :]


def rope_tables(tpos):
    T = len(tpos)
    ar = _angles(tpos // GRID_W, 64, 10000.0)
    ac = _angles(tpos % GRID_W, 64, 10000.0)
    at = _angles(tpos, 32, 500000.0)
    cosA = np.zeros((128, T), np.float32); sinA = np.zeros((128, T), np.float32)
    cosA[0:32] = np.cos(ar).T; cosA[32:64] = np.cos(ar).T
    cosA[64:96] = np.cos(ac).T; cosA[96:128] = np.cos(ac).T
    sinA[0:32] = -np.sin(ar).T; sinA[32:64] = np.sin(ar).T
    sinA[64:96] = -np.sin(ac).T; sinA[96:128] = np.sin(ac).T
    cosC = np.ones((128, T), np.float32); sinC = np.zeros((128, T), np.float32)
    cosC[0:16] = np.cos(at).T; cosC[16:32] = np.cos(at).T
    sinC[0:16] = -np.sin(at).T; sinC[16:32] = np.sin(at).T
    return cosA, sinA, cosC, sinC


def perm_mats():
    RA = np.zeros((128, 128), np.float32); RC = np.zeros((128, 128), np.float32)
    for dp in range(128):
        base = (dp // 64) * 64; o = dp - base
        pa = base + (o + 32) % 64
        RA[pa, dp] = 1.0
        if dp < 32:
            RC[(dp + 16) % 32, dp] = 1.0
    return RA, RC


def build_A(cfg):
    nc = bass.Bass("TRN2", target_bir_lowering=False)
    D, KC, TOK, NG = cfg.D, cfg.KC, cfg.TOK, cfg.NG
    NT = TOK // 512
    x = nc.dram_tensor("x", [TOK, D], F32, kind="ExternalInput").ap()
    w_r = nc.dram_tensor("w_r", [NG, 128, KC * 512], F32, kind="ExternalInput").ap()
    gpre = nc.dram_tensor("gpre", [1, D], F32, kind="ExternalInput").ap()
    tabs = nc.dram_tensor("tabs", [4, 128, TOK], F32, kind="ExternalInput").ap()
    cst = nc.dram_tensor("cst", [128, 4 * 128 + 2], F32, kind="ExternalInput").ap()
    FT = nc.dram_tensor("FT", [cfg.NF * 128, TOK], BF16, kind="ExternalOutput").ap()
    VO = nc.dram_tensor("VO", [TOK, cfg.NV * 128], BF16, kind="ExternalOutput").ap()
    wb = nc.dram_tensor("wb", [NG, 128, KC * 512], BF16, kind="Internal").ap()

    with ExitStack() as es:
        k = K(nc, es)
        pe, act, dve, pool, sp = k.pe, k.act, k.dve, k.pool, k.sp
        cstf = k.sb("cstf", [128, 4 * 128 + 2], F32)
        cstb = k.sb("cstb", [128, 4 * 128], BF16)
        gpb = k.sb("gpb", [128, D], F32)
        nhalf = k.sb("nhalf", [128, 512], F32)
        xt = k.sb("xt", [128, D], F32)
        junk = k.sb("junk", [128, D], BF16)
        hb = k.sb("hb", [128, D], BF16)
        ss1 = k.sb("ss1", [128, 1], F32); v1 = k.sb("v1", [128, 1], F32); r1 = k.sb("r1", [128, 1], F32)
        hT = k.sb("hT", [128, KC, 512], BF16)
        wt = [k.sb("wt%d" % i, [128, KC, 512], BF16) for i in range(2)]
        tb = k.sb("tb", [128, 4, 512], F32)
        st = [k.sb("st%d" % i, [128, 512], BF16) for i in range(4)]
        xb = [k.sb("xb%d" % i, [128, 512], BF16) for i in range(2)]
        sq = [k.sb("sq%d" % i, [128, 512], BF16) for i in range(2)]
        ta = k.sb("ta", [128, 512], F32); tbb = k.sb("tbb", [128, 512], F32)
        tv = k.sb("tv", [128, 512], F32); trs = k.sb("trs", [128, 512], F32)
        P = [k.ps("P%d" % i, [128, 512], F32) for i in range(3)]
        SS = k.ps("SS", [128, 512], F32); ROT = k.ps("ROT", [128, 512], F32)
        TR = [k.ps("TR%d" % i, [128, 8, 128], BF16) for i in range(2)]
        wbt = [Tk() for _ in range(NG)]
        castsem = [Tk() for _ in range(4)]

        k.dma(sp, cstf.ap[:], cst, cstf, writes=[cstf])
        k.dma(sp, gpb.ap[:], gpre.rearrange("o d -> (o d)").partition_broadcast(128), gpb, writes=[gpb])
        k.op(dve, lambda: nc.vector.tensor_copy(out=cstb.ap[:], in_=cstf.ap[:, 0:512]), reads=[cstf], writes=[cstb])
        k.op(pool, lambda: nc.gpsimd.memset(nhalf.ap[:], -0.5), writes=[nhalf])
        RAb = cstb.ap[:, 0:128]; RCb = cstb.ap[:, 128:256]; onesb = cstb.ap[:, 256:384]; identb = cstb.ap[:, 384:512]
        gq = cstf.ap[:, 512:513]; gk = cstf.ap[:, 513:514]

        for g in range(NG):
            cs = castsem[g % 4]
            if g >= 4:
                pool.wait((cs, cs.cnt))
            bb = 2048 if (KC * 512) % 2048 == 0 else 512
            src = w_r[g].rearrange("p (a b) -> (p a) b", b=bb)
            dst = wb[g].rearrange("p (a b) -> (p a) b", b=bb)
            k.dma(pool, dst, src, cs, writes=[wbt[g]])

        nload = [0]

        def load_w(g):
            s = wt[nload[0] % 2]; nload[0] += 1
            k.dma(sp, s.ap[:], wb[g].rearrange("p (a b) -> p a b", b=512), s, reads=[wbt[g]], writes=[s])
            return s

        chunkctr = [0]; ropectr = [0]; trctr = [0]
        pending = []

        def flush_post():
            while pending:
                pending.pop(0)()

        for i in range(NT):
            t0 = i * 512
            k.dma(sp, tb.ap[:], tabs[:, :, t0:t0 + 512].rearrange("f p t -> p f t"), tb, writes=[tb])
            nxt = load_w(0)
            for sub in range(4):
                r0 = t0 + sub * 128
                k.dma(sp, xt.ap[:], x[r0:r0 + 128, :], xt, writes=[xt])
                k.op(act, lambda: nc.scalar.activation(out=junk.ap[:], in_=xt.ap[:], func=AF.Square, accum_out=ss1.ap[:]),
                     reads=[xt], writes=[ss1])
                k.op(dve, lambda: nc.vector.tensor_scalar(out=v1.ap[:], in0=ss1.ap[:], scalar1=1.0 / D, scalar2=EPS,
                                                          op0=ALU.mult, op1=ALU.add), reads=[ss1], writes=[v1])
                k.op(pool, lambda: nc.gpsimd.tensor_tensor(out=r1.ap[:], in0=v1.ap[:], in1=nhalf.ap[:, 0:1], op=ALU.pow),
                     reads=[v1, nhalf], writes=[r1])
                k.op(dve, lambda: nc.vector.scalar_tensor_tensor(out=hb.ap[:], in0=xt.ap[:], scalar=r1.ap[:], in1=gpb.ap[:],
                                                                 op0=ALU.mult, op1=ALU.mult), reads=[xt, r1, gpb], writes=[hb])
                for b0 in range(0, KC, 8):
                    nb = min(8, KC - b0)
                    trp = TR[trctr[0] % 2]; trctr[0] += 1
                    k.pe_group([(lambda j=j: nc.tensor.transpose(trp.ap[:, j, :], hb.ap[:, (b0 + j) * 128:(b0 + j + 1) * 128], identb))
                                for j in range(nb)], reads=[hb, cstb], writes=[trp])
                    E = act if (trctr[0] % 2) else dve
                    if E is act:
                        k.op(act, lambda: nc.scalar.copy(out=hT.ap[:, b0:b0 + nb, sub * 128:(sub + 1) * 128], in_=trp.ap[:, 0:nb, :]),
                             reads=[trp], writes=[hT])
                    else:
                        k.op(dve, lambda: nc.vector.tensor_copy(out=hT.ap[:, b0:b0 + nb, sub * 128:(sub + 1) * 128], in_=trp.ap[:, 0:nb, :]),
                             reads=[trp], writes=[hT])
            for g in range(NG):
                wcur = nxt
                if g + 1 < NG:
                    nxt = load_w(g + 1)
                c = 0
                while c < 4:
                    ch = g * 4 + c
                    role = cfg.roles[ch]
                    if role == "V":
                        c1 = c
                        while c1 < 4 and cfg.roles[g * 4 + c1] == "V":
                            c1 += 1
                        wdt = (c1 - c) * 128
                        for sub in range(4):
                            n = chunkctr[0]; chunkctr[0] += 1
                            Pn = P[n % 3]; stn = st[n % 4]
                            k.pe_group([(lambda kc=kc: nc.tensor.matmul(Pn.ap[:, 0:wdt], lhsT=hT.ap[:, kc, sub * 128:(sub + 1) * 128],
                                                                        rhs=wcur.ap[:, kc, c * 128:c1 * 128], start=(kc == 0), stop=(kc == KC - 1)))
                                        for kc in range(KC)], reads=[hT, wcur], writes=[Pn])
                            k.op(dve, lambda: nc.vector.tensor_copy(out=stn.ap[:, 0:wdt], in_=Pn.ap[:, 0:wdt]), reads=[Pn], writes=[stn])
                            v0 = cfg.vidx[ch] * 128
                            k.dma(sp, VO[t0 + sub * 128:t0 + (sub + 1) * 128, v0:v0 + wdt], stn.ap[:, 0:wdt], stn, reads=[stn])
                            if pending:
                                pending.pop(0)()
                        c = c1
                        continue
                    n = chunkctr[0]; chunkctr[0] += 1
                    Pn = P[n % 3]; stn = st[n % 4]
                    k.pe_group([(lambda kc=kc: nc.tensor.matmul(Pn.ap[:], lhsT=wcur.ap[:, kc, c * 128:(c + 1) * 128],
                                                                rhs=hT.ap[:, kc, :], start=(kc == 0), stop=(kc == KC - 1)))
                                for kc in range(KC)], reads=[hT, wcur], writes=[Pn])
                    if pending:
                        pending.pop(0)()
                    f0 = cfg.fidx[ch] * 128
                    outap = FT[f0:f0 + 128, t0:t0 + 512]
                    if role == "P":
                        k.op(dve, lambda: nc.vector.tensor_copy(out=stn.ap[:], in_=Pn.ap[:]), reads=[Pn], writes=[stn])
                        k.dma(sp, outap, stn.ap[:], stn, reads=[stn])
                    elif role == "G":
                        k.op(act, lambda: nc.scalar.activation(out=stn.ap[:], in_=Pn.ap[:], func=AF.Silu), reads=[Pn], writes=[stn])
                        k.dma(sp, outap, stn.ap[:], stn, reads=[stn])
                    else:
                        m = ropectr[0]; ropectr[0] += 1
                        xbm = xb[m % 2]; sqm = sq[m % 2]
                        isA = role in ("AQ", "AK")
                        if isA:
                            gv = gq if role == "AQ" else gk
                            k.op(act, lambda: nc.scalar.activation(out=sqm.ap[:], in_=Pn.ap[:], func=AF.Square), reads=[Pn], writes=[sqm])
                            k.op(act, lambda: nc.scalar.activation(out=xbm.ap[:], in_=Pn.ap[:], func=AF.Copy, scale=gv),
                                 reads=[Pn, cstf], writes=[xbm])
                        else:
                            k.op(act, lambda: nc.scalar.copy(out=xbm.ap[:], in_=Pn.ap[:]), reads=[Pn], writes=[xbm])

                        def post(isA=isA, xbm=xbm, sqm=sqm, stn=stn, outap=outap):
                            Rm = RAb if isA else RCb
                            ci, si = (0, 1) if isA else (2, 3)
                            if isA:
                                k.pe_group([lambda: nc.tensor.matmul(SS.ap[:], lhsT=onesb, rhs=sqm.ap[:], start=True, stop=True)],
                                           reads=[sqm, cstb], writes=[SS])
                            k.pe_group([lambda: nc.tensor.matmul(ROT.ap[:], lhsT=Rm, rhs=xbm.ap[:], start=True, stop=True)],
                                       reads=[xbm, cstb], writes=[ROT])
                            if isA:
                                k.op(dve, lambda: nc.vector.tensor_scalar(out=tv.ap[:], in0=SS.ap[:], scalar1=1.0 / 128, scalar2=EPS,
                                                                          op0=ALU.mult, op1=ALU.add), reads=[SS], writes=[tv])
                                k.op(pool, lambda: nc.gpsimd.tensor_tensor(out=trs.ap[:], in0=tv.ap[:], in1=nhalf.ap[:], op=ALU.pow),
                                     reads=[tv, nhalf], writes=[trs])
                            k.op(dve, lambda: nc.vector.tensor_tensor(out=ta.ap[:], in0=xbm.ap[:], in1=tb.ap[:, ci, :], op=ALU.mult),
                                 reads=[xbm, tb], writes=[ta])
                            k.op(dve, lambda: nc.vector.tensor_tensor(out=tbb.ap[:], in0=ROT.ap[:], in1=tb.ap[:, si, :], op=ALU.mult),
                                 reads=[ROT, tb], writes=[tbb])
                            if isA:
                                k.op(dve, lambda: nc.vector.tensor_tensor(out=ta.ap[:], in0=ta.ap[:], in1=tbb.ap[:], op=ALU.add),
                                     reads=[ta, tbb], writes=[ta])
                                k.op(dve, lambda: nc.vector.tensor_tensor(out=stn.ap[:], in0=ta.ap[:], in1=trs.ap[:], op=ALU.mult),
                                     reads=[ta, trs], writes=[stn])
                            else:
                                k.op(dve, lambda: nc.vector.tensor_tensor(out=stn.ap[:], in0=ta.ap[:], in1=tbb.ap[:], op=ALU.add),
                                     reads=[ta, tbb], writes=[stn])
                            k.dma(sp, outap, stn.ap[:], stn, reads=[stn])
                        pending.append(post)
                    c += 1
            flush_post()
        k.finish()
    return nc


def build_B(cfg, lam_init):
    nc = bass.Bass("TRN2", target_bir_lowering=False)
    D, KC, TOK, EM = cfg.D, cfg.KC, cfg.TOK, cfg.EM
    AH, AKV, BH, CH = cfg.AH, cfg.AKV, cfg.BH, cfg.CH
    G = AH // AKV
    NKG = AKV + 2 * CH
    NVC = AKV + 2 * CH
    NQG = AH + 2 * CH + BH
    NBS = cfg.LS // cfg.BLK_S; NBP = cfg.LP // cfg.BLK_P
    scale = 1.0 / math.sqrt(HD)
    NWG = D // 512
    CW = 128 + 128 + 512 + 4

    x = nc.dram_tensor("x", [TOK, D], F32, kind="ExternalInput").ap()
    QT = nc.dram_tensor("QT", [NQG, 128, TOK], BF16, kind="ExternalInput").ap()
    SG = nc.dram_tensor("SG", [EM, 128, TOK], BF16, kind="ExternalInput").ap()
    KTs = nc.dram_tensor("KTs", [NKG, 128, cfg.LS], BF16, kind="ExternalInput").ap()
    KTp = nc.dram_tensor("KTp", [NKG, 128, cfg.LP], BF16, kind="ExternalInput").ap()
    Vs = nc.dram_tensor("Vs", [NVC, NBS, 128, cfg.BLK_S], BF16, kind="ExternalInput").ap()
    Vp = nc.dram_tensor("Vp", [NVC, NBP, 128, cfg.BLK_P], BF16, kind="ExternalInput").ap()
    KBx = nc.dram_tensor("KBx", [cfg.NQT, 128, BH * 1024], BF16, kind="ExternalInput").ap()
    VBx = nc.dram_tensor("VBx", [cfg.NQT, 128, 8 * BH * 128], BF16, kind="ExternalInput").ap()
    TBm = nc.dram_tensor("TBm", [BH, 64, 23 * 64], F32, kind="ExternalInput").ap()
    RMK = nc.dram_tensor("RMK", [cfg.NQT, 8, 1024], BF16, kind="ExternalInput").ap()
    wo_r = nc.dram_tensor("wo_r", [NWG, 128, EM * 512], F32, kind="ExternalInput").ap()
    gpost = nc.dram_tensor("gpost", [1, D], F32, kind="ExternalInput").ap()
    cst = nc.dram_tensor("cst", [128, CW], F32, kind="ExternalInput").ap()
    lam = nc.dram_tensor("lam", [1, 4 * 128], F32, kind="ExternalInput").ap()
    XO = nc.dram_tensor("XO", [TOK, D], F32, kind="ExternalOutput").ap()
    wob = nc.dram_tensor("wob", [NWG, 128, EM * 512], BF16, kind="Internal").ap()

    with ExitStack() as es:
        k = K(nc, es)
        pe, act, dve, pool, sp = k.pe, k.act, k.dve, k.pool, k.sp
        cstf = k.sb("cstf", [128, CW], F32)
        cstb = k.sb("cstb", [128, 768], BF16)
        gpb = k.sb("gpb", [128, D], F32)
        nhalf = k.sb("nhalf", [128, 512], F32)
        mix = k.sb("mix", [128, EM, 512], BF16)
        nlam = k.sb("nlam", [128, 1], F32)
        gs = k.sb("gs", [128, 2], F32)
        lamt = k.sb("lamt", [1, 512], F32); lamp = k.sb("lamp", [1, 512], F32)
        l2 = k.sb("l2", [1, 2], F32); l3 = k.sb("l3", [1, 2], F32)

        k.dma(sp, cstf.ap[:], cst, cstf, writes=[cstf])
        k.dma(sp, gpb.ap[:], gpost.rearrange("o d -> (o d)").partition_broadcast(128), gpb, writes=[gpb])
        k.dma(sp, lamt.ap[:], lam, lamt, writes=[lamt])
        k.op(dve, lambda: nc.vector.tensor_copy(out=cstb.ap[:], in_=cstf.ap[:, 0:768]), reads=[cstf], writes=[cstb])
        k.op(pool, lambda: nc.gpsimd.memset(nhalf.ap[:], -0.5), writes=[nhalf])
        if _os_env_bskip():
            k.op(pool, lambda: nc.gpsimd.memset(mix.ap[:], 0.0), writes=[mix])
        onesb = cstb.ap[:, 0:128]; identb = cstb.ap[:, 128:256]
        selb = cstb.ap[0:8, 256:768]
        import os as _os
        if 'l' not in _os.environ.get('BSKIP', ''):
            k.op(dve, lambda: nc.vector.tensor_tensor(out=lamp.ap[:, 0:256], in0=lamt.ap[:, 0:256], in1=lamt.ap[:, 256:512], op=ALU.mult),
                 reads=[lamt], writes=[lamp])
            k.op(dve, lambda: nc.vector.tensor_reduce(out=l2.ap[:], in_=lamp.ap[:, 0:256].rearrange("o (a b) -> o a b", b=128),
                                                      op=ALU.add, axis=mybir.AxisListType.X), reads=[lamp], writes=[l2])
            k.op(act, lambda: nc.scalar.activation(out=l3.ap[:], in_=l2.ap[:], func=AF.Exp), reads=[l2], writes=[l3])
            k.op(dve, lambda: nc.vector.scalar_tensor_tensor(out=l2.ap[:, 0:1], in0=l3.ap[:, 1:2], scalar=cstf.ap[0:1, 770:771], in1=l3.ap[:, 0:1],
                                                             op0=ALU.add, op1=ALU.subtract), reads=[l3, l2], writes=[l2])
            k.op(dve, lambda: nc.vector.tensor_scalar(out=gs.ap[:], in0=cstf.ap[:, 768:770], scalar1=cstf.ap[:, 771:772], scalar2=None,
                                                      op0=ALU.mult), reads=[cstf], writes=[gs])


        wobt = [Tk() for _ in range(NWG)]
        castsem = [Tk() for _ in range(4)]
        for g in range(NWG):
            cs = castsem[g % 4]
            if g >= 4:
                pool.wait((cs, cs.cnt))
            bb = 2048 if (EM * 512) % 2048 == 0 else 512
            src = wo_r[g].rearrange("p (a b) -> (p a) b", b=bb)
            dst = wob[g].rearrange("p (a b) -> (p a) b", b=bb)
            k.dma(pool, dst, src, cs, writes=[wobt[g]])

        S = [k.ps("S%d" % i, [128, 512], F32) for i in range(3)]
        Oa = k.ps("Oa", [128, 512], F32); Ob = k.ps("Ob", [128, 512], F32); Lp = k.ps("Lp", [128, 512], F32)
        MISC = k.ps("MISC", [128, 512], F32)
        ptps = k.ps("ptps", [128, 8, 64], BF16)
        lscr = nc.dram_tensor("lscr", [1, 2], F32, kind="Internal").ap()
        lscr_t = Tk()
        if 'l' not in _os.environ.get('BSKIP', ''):
            k.dma(sp, lscr, l2.ap[:], lscr_t, reads=[l2], writes=[lscr_t])
            k.dma(sp, nlam.ap[:], lscr[0, 0:1].partition_broadcast(128), nlam, reads=[lscr_t], writes=[nlam])

        Kr = [Tk() for _ in range(4)]; Vr = [Tk() for _ in range(4)]; PT = [Tk() for _ in range(4)]
        qin = [Tk() for _ in range(2)]; sgin = [Tk() for _ in range(2)]
        rl, o1, oo, sqb, tt, tv, rs = [Tk() for _ in range(7)]
        KB, VB, rmk, sbs, pb, pbn, mx, sm, rsm, ptb = [Tk() for _ in range(10)]
        TBt = [Tk() for _ in range(2)]
        wr = [Tk() for _ in range(2)]; yall, xt, junk, ssp, ss1, v1, r1 = [Tk() for _ in range(7)]
        nchunk = [0]; ntask = [0]; nwl = [0]; yc = [0]

        for qt in range(cfg.NQT):
            is_s = qt < cfg.NQT_S
            t0 = qt * 512
            KTd = KTs if is_s else KTp
            Vd = Vs if is_s else Vp
            BLK = cfg.BLK_S if is_s else cfg.BLK_P
            NB = NBS if is_s else NBP
            NJ = BLK // 128
            k.barrier()
            with ExitStack() as es1:
                sfx = "_%d" % qt
                for i in range(4):
                    k.scoped(Kr[i], es1, "Kr%d%s" % (i, sfx), [128, BLK], BF16)
                    k.scoped(Vr[i], es1, "Vr%d%s" % (i, sfx), [128, NJ, 256], BF16)
                for i in range(4):
                    k.scoped(PT[i], es1, "PT%d%s" % (i, sfx), [128, 512], BF16)
                for i in range(2):
                    k.scoped(qin[i], es1, "qin%d%s" % (i, sfx), [128, 512], BF16)
                    k.scoped(sgin[i], es1, "sgin%d%s" % (i, sfx), [128, 2, 512], BF16)
                    k.scoped(TBt[i], es1, "TBt%d%s" % (i, sfx), [64, 23 * 64], F32)
                k.scoped(rl, es1, "rl" + sfx, [128, 512], F32); k.scoped(o1, es1, "o1" + sfx, [128, 2, 512], F32)
                k.scoped(oo, es1, "oo" + sfx, [128, 2, 512], F32); k.scoped(sqb, es1, "sqb" + sfx, [128, 2, 512], BF16)
                k.scoped(tt, es1, "tt" + sfx, [128, 512], F32); k.scoped(tv, es1, "tv" + sfx, [128, 512], F32)
                k.scoped(rs, es1, "rs" + sfx, [128, 512], F32)
                k.scoped(KB, es1, "KB" + sfx, [128, BH, 1024], BF16); k.scoped(VB, es1, "VB" + sfx, [128, 8, BH * 128], BF16)
                k.scoped(rmk, es1, "rmk" + sfx, [8, 1024], BF16)
                k.scoped(sbs, es1, "sbs" + sfx, [64, 1024], F32); k.scoped(pb, es1, "pb" + sfx, [64, 1024], BF16)
                k.scoped(pbn, es1, "pbn" + sfx, [64, 1024], BF16)
                k.scoped(mx, es1, "mx" + sfx, [64, 1], F32); k.scoped(sm, es1, "sm" + sfx, [64, 1], F32); k.scoped(rsm, es1, "rsm" + sfx, [64, 1], F32)
                k.scoped(ptb, es1, "ptb" + sfx, [128, 8, 64], BF16)

                k.dma(sp, KB.ap[:], KBx[qt].rearrange("p (h t) -> p h t", t=1024), KB, writes=[KB])
                k.dma(sp, VB.ap[:], VBx[qt].rearrange("p (b c) -> p b c", c=BH * 128), VB, writes=[VB])
                k.dma(sp, rmk.ap[:], RMK[qt], rmk, writes=[rmk])

                jobs = []
                for h in range(AH):
                    jobs.append(("A", h, h, h // G, [h // G]))
                for m in range(CH):
                    for j in range(2):
                        jobs.append(("C%d" % j, AH + 2 * m + j, AH + BH + 2 * m, AKV + 2 * m + j, [AKV + 2 * m, AKV + 2 * m + 1]))
                import os as _os
                _skip = _os.environ.get("BSKIP", "")
                if "a" in _skip:
                    jobs = []
                if "c" in _skip:
                    jobs = [j_ for j_ in jobs if j_[0] == "A"]
                for ji, (kind, qg, mc0, kg, vcs) in enumerate(jobs):
                    nv = len(vcs)
                    qi = qin[ji % 2]; sgi = sgin[ji % 2]
                    k.dma(sp, qi.ap[:], QT[qg, :, t0:t0 + 512], qi, writes=[qi])
                    if kind != "C0":
                        k.dma(sp, sgi.ap[:, 0:nv, :], SG[mc0:mc0 + nv, :, t0:t0 + 512].rearrange("c p t -> p c t"), sgi, writes=[sgi])

                    def load_blk(b, kg=kg, vcs=vcs):
                        s = ntask[0] % 4; ntask[0] += 1
                        k.dma(sp, Kr[s].ap[:], KTd[kg, :, b * BLK:(b + 1) * BLK], Kr[s], writes=[Kr[s]])
                        for vi, vc in enumerate(vcs):
                            tokv = k.dma(sp, Vr[s].ap[:, :, vi * 128:(vi + 1) * 128], Vd[vc, b].rearrange("p (j d) -> p j d", d=128), Vr[s],
                                         writes=([Vr[s]] if vi == 0 else []))
                            Vr[s].w = tokv
                        return s
                    slots = [load_blk(b) for b in range(min(2, NB))]
                    pend = []
                    first = [True]

                    def do_pv(s, j, n, last, nv=nv):
                        ptn = PT[n % 4]
                        st_ = first[0]
                        fns = [lambda: nc.tensor.matmul(Oa.ap[:], lhsT=Vr[s].ap[:, j, 0:128], rhs=ptn.ap[:], start=st_, stop=last)]
                        if nv == 2:
                            fns.append(lambda: nc.tensor.matmul(Ob.ap[:], lhsT=Vr[s].ap[:, j, 128:256], rhs=ptn.ap[:], start=st_, stop=last))
                        fns.append(lambda: nc.tensor.matmul(Lp.ap[:], lhsT=onesb, rhs=ptn.ap[:], start=st_, stop=last))
                        wrt = [Oa, Lp] + ([Ob] if nv == 2 else [])
                        k.pe_group(fns, reads=[Vr[s], ptn, cstb], writes=wrt)
                        first[0] = False

                    total = NB * NJ; cnt = 0
                    for b in range(NB):
                        s = slots[b]
                        if b + 2 < NB:
                            slots.append(load_blk(b + 2))
                        for j in range(NJ):
                            n = nchunk[0]; nchunk[0] += 1
                            Sn = S[n % 3]; ptn = PT[n % 4]
                            k.pe_group([lambda: nc.tensor.matmul(Sn.ap[:], lhsT=Kr[s].ap[:, j * 128:(j + 1) * 128], rhs=qi.ap[:], start=True, stop=True)],
                                       reads=[Kr[s], qi], writes=[Sn])
                            k.op(act, lambda: nc.scalar.activation(out=ptn.ap[:], in_=Sn.ap[:], func=AF.Exp, scale=scale), reads=[Sn], writes=[ptn])
                            cnt += 1
                            pend.append((s, j, n, cnt == total))
                            if len(pend) > 2:
                                do_pv(*pend.pop(0))
                    while pend:
                        do_pv(*pend.pop(0))
                    k.op(dve, lambda: nc.vector.reciprocal(out=rl.ap[:], in_=Lp.ap[:]), reads=[Lp], writes=[rl])
                    if kind == "A":
                        k.op(dve, lambda: nc.vector.tensor_tensor(out=tt.ap[:], in0=Oa.ap[:], in1=rl.ap[:], op=ALU.mult), reads=[Oa, rl], writes=[tt])
                        k.op(dve, lambda: nc.vector.tensor_tensor(out=mix.ap[:, mc0, :], in0=tt.ap[:], in1=sgi.ap[:, 0, :], op=ALU.mult),
                             reads=[tt, sgi], writes=[mix])
                    elif kind == "C0":
                        k.op(dve, lambda: nc.vector.tensor_tensor(out=o1.ap[:, 0, :], in0=Oa.ap[:], in1=rl.ap[:], op=ALU.mult), reads=[Oa, rl], writes=[o1])
                        k.op(dve, lambda: nc.vector.tensor_tensor(out=o1.ap[:, 1, :], in0=Ob.ap[:], in1=rl.ap[:], op=ALU.mult), reads=[Ob, rl, o1], writes=[o1])
                    else:
                        for v, Ov in enumerate((Oa, Ob)):
                            k.op(dve, lambda: nc.vector.tensor_tensor(out=tt.ap[:], in0=Ov.ap[:], in1=rl.ap[:], op=ALU.mult), reads=[Ov, rl], writes=[tt])
                            k.op(dve, lambda: nc.vector.scalar_tensor_tensor(out=oo.ap[:, v, :], in0=tt.ap[:], scalar=nlam.ap[:], in1=o1.ap[:, v, :],
                                                                             op0=ALU.mult, op1=ALU.add), reads=[tt, nlam, o1, oo], writes=[oo])
                        k.op(pool, lambda: nc.gpsimd.tensor_tensor(out=sqb.ap[:], in0=oo.ap[:], in1=oo.ap[:], op=ALU.mult), reads=[oo], writes=[sqb])
                        k.pe_group([lambda: nc.tensor.matmul(MISC.ap[:], lhsT=onesb, rhs=sqb.ap[:, 0, :], start=True, stop=False),
                                    lambda: nc.tensor.matmul(MISC.ap[:], lhsT=onesb, rhs=sqb.ap[:, 1, :], start=False, stop=True)],
                                   reads=[sqb, cstb], writes=[MISC])
                        k.op(dve, lambda: nc.vector.tensor_scalar(out=tv.ap[:], in0=MISC.ap[:], scalar1=1.0 / 256, scalar2=EPS, op0=ALU.mult, op1=ALU.add),
                             reads=[MISC], writes=[tv])
                        k.op(pool, lambda: nc.gpsimd.tensor_tensor(out=rs.ap[:], in0=tv.ap[:], in1=nhalf.ap[:], op=ALU.pow), reads=[tv, nhalf], writes=[rs])
                        for v in range(2):
                            k.op(dve, lambda: nc.vector.scalar_tensor_tensor(out=tt.ap[:], in0=oo.ap[:, v, :], scalar=gs.ap[:, v:v + 1], in1=rs.ap[:],
                                                                             op0=ALU.mult, op1=ALU.mult), reads=[oo, gs, rs], writes=[tt])
                            k.op(dve, lambda: nc.vector.tensor_tensor(out=mix.ap[:, mc0 + v, :], in0=tt.ap[:], in1=sgi.ap[:, v, :], op=ALU.mult),
                                 reads=[tt, sgi], writes=[mix])

                Sb = [S[0], S[1]]; Ops = S[2]
                for h in range(0 if "n" in _skip else BH):
                    tbt = TBt[h % 2]
                    k.dma(sp, tbt.ap[:], TBm[h], tbt, writes=[tbt])
                    qi = qin[h % 2]; sgi = sgin[h % 2]
                    k.dma(sp, qi.ap[:], QT[AH + 2 * CH + h, :, t0:t0 + 512], qi, writes=[qi])
                    k.dma(sp, sgi.ap[:, 0, :], SG[AH + h, :, t0:t0 + 512], sgi, writes=[sgi])
                    for j in range(8):
                        for half in range(2):
                            k.pe_group([lambda: nc.tensor.matmul(Sb[half].ap[0:64, :], lhsT=qi.ap[:, j * 64:(j + 1) * 64],
                                                                 rhs=KB.ap[:, h, half * 512:(half + 1) * 512], start=True, stop=False),
                                        lambda: nc.tensor.matmul(Sb[half].ap[0:64, :], lhsT=selb[:, j * 64:(j + 1) * 64], rhs=rmk.ap[:, half * 512:(half + 1) * 512],
                                                                 start=False, stop=True)],
                                       reads=[qi, KB, rmk, cstb], writes=[Sb[half]])
                            k.op(dve, lambda: nc.vector.scalar_tensor_tensor(out=sbs.ap[:, half * 512:(half + 1) * 512], in0=Sb[half].ap[0:64, :], scalar=scale,
                                                                             in1=tbt.ap[:, (7 - j) * 64 + half * 512:(7 - j) * 64 + (half + 1) * 512],
                                                                             op0=ALU.mult, op1=ALU.add), reads=[Sb[half], tbt, sbs], writes=[sbs])
                        k.op(dve, lambda: nc.vector.tensor_reduce(out=mx.ap[:], in_=sbs.ap[:], op=ALU.max, axis=mybir.AxisListType.X, negate=True),
                             reads=[sbs], writes=[mx])
                        k.op(act, lambda: nc.scalar.activation(out=pb.ap[:], in_=sbs.ap[:], func=AF.Exp, bias=mx.ap[:], accum_out=sm.ap[:]),
                             reads=[sbs, mx], writes=[pb, sm])
                        k.op(dve, lambda: nc.vector.reciprocal(out=rsm.ap[:], in_=sm.ap[:]), reads=[sm], writes=[rsm])
                        k.op(dve, lambda: nc.vector.tensor_scalar(out=pbn.ap[:], in0=pb.ap[:], scalar1=rsm.ap[:], scalar2=None, op0=ALU.mult),
                             reads=[pb, rsm], writes=[pbn])
                        k.pe_group([(lambda m=m: nc.tensor.transpose(ptps.ap[:, m, :], pbn.ap[:, m * 128:(m + 1) * 128], identb[0:64, 0:64]))
                                    for m in range(8)], reads=[pbn, cstb], writes=[ptps])
                        k.op(act, lambda: nc.scalar.copy(out=ptb.ap[:], in_=ptps.ap[:]), reads=[ptps], writes=[ptb])
                        k.pe_group([(lambda m=m: nc.tensor.matmul(Ops.ap[:, 0:64], lhsT=VB.ap[:, m, h * 128:(h + 1) * 128], rhs=ptb.ap[:, m, :],
                                                                  start=(m == 0), stop=(m == 7))) for m in range(8)], reads=[VB, ptb], writes=[Ops])
                        k.op(dve, lambda: nc.vector.tensor_tensor(out=mix.ap[:, AH + h, j * 64:(j + 1) * 64], in0=Ops.ap[:, 0:64],
                                                                  in1=sgi.ap[:, 0, j * 64:(j + 1) * 64], op=ALU.mult), reads=[Ops, sgi], writes=[mix])
                k.barrier()
            with ExitStack() as es2:
                sfx = "_w%d" % qt
                for i in range(2):
                    k.scoped(wr[i], es2, "wr%d%s" % (i, sfx), [128, EM, 512], BF16)
                k.scoped(yall, es2, "yall" + sfx, [128, 2, D], F32)
                k.scoped(xt, es2, "xt" + sfx, [128, D], F32)
                k.scoped(junk, es2, "junk" + sfx, [128, D], BF16)
                k.scoped(ssp, es2, "ssp" + sfx, [128, 2, NWG], F32)
                k.scoped(ss1, es2, "ss1" + sfx, [128, 1], F32); k.scoped(v1, es2, "v1" + sfx, [128, 1], F32); k.scoped(r1, es2, "r1" + sfx, [128, 1], F32)

                def load_wo(g):
                    s = wr[nwl[0] % 2]; nwl[0] += 1
                    k.dma(sp, s.ap[:], wob[g].rearrange("p (a b) -> p a b", b=512), s, reads=[wobt[g]], writes=[s])
                    return s
                for hq in range(0 if "w" in _os.environ.get("BSKIP", "") else 2):
                    nxt = load_wo(0)
                    for g in range(NWG):
                        wcur = nxt
                        if g + 1 < NWG:
                            nxt = load_wo(g + 1)
                        for s2 in range(0 if "m" in _os.environ.get("BSKIP", "") else 2):
                            sub = hq * 2 + s2
                            Y = S[yc[0] % 3]; yc[0] += 1
                            k.pe_group([(lambda e=e: nc.tensor.matmul(Y.ap[:], lhsT=mix.ap[:, e, sub * 128:(sub + 1) * 128], rhs=wcur.ap[:, e, :],
                                                                      start=(e == 0), stop=(e == EM - 1))) for e in range(EM)],
                                       reads=[mix, wcur], writes=[Y])
                            if "y" not in _os.environ.get("BSKIP", ""):
                                k.op(dve, lambda: nc.vector.tensor_copy(out=yall.ap[:, s2, g * 512:(g + 1) * 512], in_=Y.ap[:]), reads=[Y, yall], writes=[yall])
                    for s2 in range(0 if "p" in _os.environ.get("BSKIP", "") else 2):
                        sub = hq * 2 + s2
                        r0 = t0 + sub * 128
                        k.dma(sp, xt.ap[:], x[r0:r0 + 128, :], xt, writes=[xt])
                        k.op(act, lambda: nc.scalar.activation(out=junk.ap[:], in_=yall.ap[:, s2, :], func=AF.Square, accum_out=ss1.ap[:]),
                             reads=[yall], writes=[ss1])
                        k.op(dve, lambda: nc.vector.tensor_scalar(out=v1.ap[:], in0=ss1.ap[:], scalar1=1.0 / D, scalar2=EPS, op0=ALU.mult, op1=ALU.add),
                             reads=[ss1], writes=[v1])
                        k.op(pool, lambda: nc.gpsimd.tensor_tensor(out=r1.ap[:], in0=v1.ap[:], in1=nhalf.ap[:, 0:1], op=ALU.pow), reads=[v1, nhalf], writes=[r1])
                        k.op(dve, lambda: nc.vector.scalar_tensor_tensor(out=yall.ap[:, s2, :], in0=yall.ap[:, s2, :], scalar=r1.ap[:], in1=gpb.ap[:],
                                                                         op0=ALU.mult, op1=ALU.mult), reads=[yall, r1, gpb], writes=[yall])
                        k.op(dve, lambda: nc.vector.tensor_tensor(out=xt.ap[:], in0=xt.ap[:], in1=yall.ap[:, s2, :], op=ALU.add), reads=[xt, yall], writes=[xt])
                        k.dma(sp, XO[r0:r0 + 128, :], xt.ap[:], xt, reads=[xt])
                k.barrier()
        k.finish()
    return nc


_PROGS = {}


def _prog(name, cfg, fn):
    key = (name, cfg.D, cfg.AH, cfg.AKV, cfg.BH, cfg.CH, cfg.NC, cfg.S_OWN, cfg.P_OWN)
    if key not in _PROGS:
        _PROGS[key] = fn(cfg)
    return _PROGS[key]


def _build_B(cfg):
    return build_B(cfg, None)


def _tbm(cfg, rel_bias_l):
    q = np.arange(64)[:, None]; kk = np.arange(64)[None, :]
    cs = np.clip(q - 8, 0, GRID_W - 16)
    col_ok = (kk >= cs) & (kk < cs + 16)
    dc = np.clip(kk - q, -15, 15) + 15
    out = np.full((cfg.BH, 64, 23, 64), NEG, np.float32)
    for dri in range(23):
        dr = dri - 4
        if 0 <= dr < 15:
            g = rel_bias_l[:, dr, :][:, dc]
            out[:, :, dri, :] = np.where(col_ok[None], g, np.float32(NEG))
    return np.ascontiguousarray(out.reshape(cfg.BH, 64, 23 * 64))


def _seq_geom(cfg, c, qt):
    if qt < cfg.NQT_S:
        return True, cfg.LS // 64, (c * cfg.S_OWN + qt * 512) // 64
    return False, cfg.LP // 64, ((c % 2) * cfg.P_OWN + (qt - cfg.NQT_S) * 512) // 64


def run_layer(cfg, xs, P, l, dbg=None):
    NC = cfg.NC; D = cfg.D; KC = cfg.KC
    cores = list(range(NC))
    w_in = P["w_in"][l]
    w_r = np.ascontiguousarray(w_in.reshape(KC, 128, cfg.NG, 512).transpose(2, 1, 0, 3)).reshape(cfg.NG, 128, KC * 512)
    RA, RC = perm_mats()
    cstA = np.zeros((128, 514), np.float32)
    cstA[:, 0:128] = RA; cstA[:, 128:256] = RC; cstA[:, 256:384] = 1.0; cstA[:, 384:512] = np.eye(128, dtype=np.float32)
    cstA[:, 512] = P["a_q_norm_g"][l]; cstA[:, 513] = P["a_k_norm_g"][l]
    gpre = np.ascontiguousarray(P["pre_norm_g"][l][None, :])
    in_maps = []
    for c in cores:
        pos = np.concatenate([c * cfg.S_OWN + np.arange(cfg.S_OWN), (c % 2) * cfg.P_OWN + np.arange(cfg.P_OWN)])
        tabs = np.stack(rope_tables(pos), 0)
        in_maps.append({"x": xs[c], "w_r": w_r, "gpre": gpre, "tabs": np.ascontiguousarray(tabs), "cst": cstA})
    resA = run_bass_kernel_spmd(_prog("A", cfg, build_A), in_maps, core_ids=cores).results
    FT = [np.asarray(r["FT"]).reshape(cfg.NF, 128, cfg.TOK) for r in resA]
    VO = [np.asarray(r["VO"]) for r in resA]
    if dbg is not None:
        dbg["FT"] = FT; dbg["VO"] = VO
        if dbg.get("stopA"):
            return xs
    del w_r
    AH, AKV, BH, CH = cfg.AH, cfg.AKV, cfg.BH, cfg.CH
    f = cfg.fidx
    qa = [f[cfg.c_qa + i] for i in range(AH)]; ka = [f[cfg.c_ka + i] for i in range(AKV)]
    qb = [f[cfg.c_qb + i] for i in range(BH)]; kb = [f[cfg.c_kb + i] for i in range(BH)]
    qc = [f[cfg.c_qc + i] for i in range(2 * CH)]; kc = [f[cfg.c_kc + i] for i in range(2 * CH)]
    gg = [f[cfg.c_g + i] for i in range(cfg.EM)]
    va = [cfg.vidx[cfg.c_va + i] for i in range(AKV)]; vb = [cfg.vidx[cfg.c_vb + i] for i in range(BH)]
    vc = [cfg.vidx[cfg.c_vc + i] for i in range(2 * CH)]
    S = cfg.S_OWN
    kgrp = ka + kc; vgrp = va + vc
    KTs = np.ascontiguousarray(np.concatenate([FT[c][kgrp][:, :, :S] for c in cores], axis=2))
    NBS = cfg.LS // cfg.BLK_S; NJS = cfg.BLK_S // 128
    Vs_seq = np.concatenate([VO[c][:S].reshape(S, cfg.NV, 128)[:, vgrp] for c in cores], axis=0)
    Vs = np.ascontiguousarray(Vs_seq.reshape(NBS, NJS, 128, len(vgrp), 128).transpose(3, 0, 2, 1, 4)).reshape(len(vgrp), NBS, 128, cfg.BLK_S)
    KBs = np.concatenate([FT[c][kb][:, :, :S] for c in cores], axis=2)
    VBs = np.concatenate([VO[c][:S].reshape(S, cfg.NV, 128)[:, vb] for c in cores], axis=0)
    NBP = cfg.LP // cfg.BLK_P; NJP = cfg.BLK_P // 128
    pairK = {}; pairV = {}; pairKB = {}; pairVB = {}
    for b in range(NC // 2):
        cc = [2 * b, 2 * b + 1]
        pairK[b] = np.ascontiguousarray(np.concatenate([FT[c][kgrp][:, :, S:] for c in cc], axis=2))
        vseq = np.concatenate([VO[c][S:].reshape(cfg.P_OWN, cfg.NV, 128)[:, vgrp] for c in cc], axis=0)
        pairV[b] = np.ascontiguousarray(vseq.reshape(NBP, NJP, 128, len(vgrp), 128).transpose(3, 0, 2, 1, 4)).reshape(len(vgrp), NBP, 128, cfg.BLK_P)
        pairKB[b] = np.concatenate([FT[c][kb][:, :, S:] for c in cc], axis=2)
        pairVB[b] = np.concatenate([VO[c][S:].reshape(cfg.P_OWN, cfg.NV, 128)[:, vb] for c in cc], axis=0)
    w_out = P["w_out"][l]
    NWG = D // 512
    wo_r = np.ascontiguousarray(w_out.reshape(cfg.EM, 128, NWG, 512).transpose(2, 1, 0, 3)).reshape(NWG, 128, cfg.EM * 512)
    lam_init = 0.8 - 0.6 * math.exp(-0.3 * l)
    cstB = np.zeros((128, 772), np.float32)
    cstB[:, 0:128] = 1.0; cstB[:, 128:256] = np.eye(128, dtype=np.float32)
    for j in range(8):
        cstB[j, 256 + j * 64:256 + (j + 1) * 64] = 1.0
    cstB[:, 768] = P["c_subln_g"][l][:128]; cstB[:, 769] = P["c_subln_g"][l][128:]
    cstB[:, 770] = -lam_init; cstB[:, 771] = 1.0 - lam_init
    lamv = np.concatenate([P["c_lambda_q1"][l], P["c_lambda_q2"][l], P["c_lambda_k1"][l], P["c_lambda_k2"][l]])[None, :].astype(np.float32)
    TBm = _tbm(cfg, P["b_rel_bias"][l])
    gpost = np.ascontiguousarray(P["post_norm_g"][l][None, :])
    in_maps = []
    for c in cores:
        b = c // 2
        KBx = np.zeros((cfg.NQT, 128, BH, 16, 64), NPBF); VBx = np.zeros((cfg.NQT, 16, 64, BH, 128), NPBF)
        RMK = np.zeros((cfg.NQT, 8, 16, 64), np.float32)
        for qt in range(cfg.NQT):
            is_s, rows, R0 = _seq_geom(cfg, c, qt)
            kbs = KBs if is_s else pairKB[b]; vbs = VBs if is_s else pairVB[b]
            for e in range(16):
                row = R0 - 4 + e
                if 0 <= row < rows:
                    KBx[qt, :, :, e, :] = kbs[:, :, row * 64:(row + 1) * 64].transpose(1, 0, 2)
                    VBx[qt, e] = vbs[row * 64:(row + 1) * 64]
            for j in range(8):
                r = R0 + j
                rs_ = min(max(r - 4, 0), rows - 8)
                for e in range(16):
                    row = R0 - 4 + e
                    if not (rs_ <= row < rs_ + 8):
                        RMK[qt, j, e, :] = NEG
        VBx2 = np.ascontiguousarray(VBx.reshape(cfg.NQT, 8, 128, BH * 128).transpose(0, 2, 1, 3)).reshape(cfg.NQT, 128, 8 * BH * 128)
        in_maps.append({
            "x": xs[c],
            "QT": np.ascontiguousarray(FT[c][qa + qc + qb]), "SG": np.ascontiguousarray(FT[c][gg]),
            "KTs": KTs, "KTp": pairK[b], "Vs": Vs, "Vp": pairV[b],
            "KBx": KBx.reshape(cfg.NQT, 128, BH * 1024), "VBx": VBx2, "TBm": TBm,
            "RMK": RMK.reshape(cfg.NQT, 8, 1024).astype(NPBF),
            "wo_r": wo_r, "gpost": gpost, "cst": cstB, "lam": lamv,
        })
    resB = run_bass_kernel_spmd(_prog("B", cfg, _build_B), in_maps, core_ids=cores).results
    return [np.asarray(r["XO"]) for r in resB]


def run_model(cfg, x_prompt, x_sample, P, depth, dbg=None):
    NC = cfg.NC; S = cfg.S_OWN; Pn = cfg.P_OWN
    xs = []
    for c in range(NC):
        xs.append(np.ascontiguousarray(np.concatenate([x_sample[0, c * S:(c + 1) * S], x_prompt[c // 2, (c % 2) * Pn:(c % 2 + 1) * Pn]], axis=0)))
    for l in range(depth):
        xs = run_layer(cfg, xs, P, l, dbg if l == 0 else None)
    y_sample = np.concatenate([xs[c][:S] for c in range(NC)], axis=0)[None]
    y_prompt = np.stack([np.concatenate([xs[2 * b][S:], xs[2 * b + 1][S:]], axis=0) for b in range(NC // 2)], axis=0)
    return y_prompt, y_sample


def kernel(x_prompt, x_sample, pre_norm_g, post_norm_g, w_in, w_out, a_q_norm_g, a_k_norm_g,
           b_rel_bias, c_lambda_q1, c_lambda_k1, c_lambda_q2, c_lambda_k2, c_subln_g):
    P = dict(pre_norm_g=pre_norm_g, post_norm_g=post_norm_g, w_in=w_in, w_out=w_out, a_q_norm_g=a_q_norm_g,
             a_k_norm_g=a_k_norm_g, b_rel_bias=b_rel_bias, c_lambda_q1=c_lambda_q1, c_lambda_k1=c_lambda_k1,
             c_lambda_q2=c_lambda_q2, c_lambda_k2=c_lambda_k2, c_subln_g=c_subln_g)
    P = {k_: np.asarray(v, dtype=np.float32) for k_, v in P.items()}
    yp, ys = run_model(FULL, np.asarray(x_prompt, np.float32), np.asarray(x_sample, np.float32), P, depth=w_in.shape[0])
    return yp.astype(np.float32), ys.astype(np.float32)
```
